# Optimizing a Trainium2 kernel written in Bass

```python
import functools
import jax, jax.numpy as jnp
from jax import lax
import numpy as np

D_MODEL = 1024
BATCH = 4
SEQ = 4096
DEPTH = 1
DEC_BATCH = 128
DEC_SEQ = 4
PAST_LEN = 2048
PAGE_SIZE = 128

HEAD_DIM = 64
WINDOWS = (128, 512, 2048)
DILATIONS = (1, 4, 16)
N_GROUPS = 3
HEADS_PER_GROUP = 4
N_HEADS = N_GROUPS * HEADS_PER_GROUP
ATTN_WIDTH = N_HEADS * HEAD_DIM
ATTN_OUT_WIDTH = HEADS_PER_GROUP * HEAD_DIM
ATTN_SCALE = HEAD_DIM ** -0.5
ROT_DIM = HEAD_DIM // 4
ROPE_THETA = 500000.0
CONV_CH = D_MODEL // 2
CONV_WIDTH = 31
D_FF = 256 * ((8 * D_MODEL // 3 + 255) // 256)
NORM_EPS = 1e-6
IN_SPLITS = (CONV_CH, 2 * CONV_CH, 2 * CONV_CH + ATTN_WIDTH, 2 * CONV_CH + 2 * ATTN_WIDTH, 2 * CONV_CH + 3 * ATTN_WIDTH)
IN_WIDTH = 2 * CONV_CH + 3 * ATTN_WIDTH + 2 * D_MODEL

kernel_name = "macaron_conv_dilated_window_hybrid_step"


def rmsnorm(x, g):
    xf = x.astype(jnp.float32)
    y = xf * lax.rsqrt(jnp.mean(xf * xf, axis=-1, keepdims=True) + NORM_EPS)
    return (y * g.astype(jnp.float32)).astype(x.dtype)


def swiglu_ffn(x, w_in, w_out):
    gate, up = jnp.split(x @ w_in, 2, axis=-1)
    return (jax.nn.silu(gate) * up) @ w_out


def rotary(x, pos):
    half = ROT_DIM // 2
    inv = jnp.float32(ROPE_THETA) ** (-jnp.arange(half, dtype=jnp.float32) * (2.0 / ROT_DIM))
    ang = pos.astype(jnp.float32)[:, None] * inv[None, :]
    cos = jnp.cos(ang)[:, None, :]
    sin = jnp.sin(ang)[:, None, :]
    xf = x.astype(jnp.float32)
    x1 = xf[..., :half]
    x2 = xf[..., half:ROT_DIM]
    out = jnp.concatenate([x1 * cos - x2 * sin, x2 * cos + x1 * sin, xf[..., ROT_DIM:]], axis=-1)
    return out.astype(x.dtype)


def mix_project(h, pos, w_in, b_gate, q_norm, k_norm):
    u_a, u_b, q, k, v, gates = jnp.split(h @ w_in, list(IN_SPLITS), axis=-1)
    u = u_a * jax.nn.sigmoid(u_b)
    shp = h.shape[:-1] + (N_HEADS, HEAD_DIM)
    q = rotary(rmsnorm(q.reshape(shp), q_norm), pos)
    k = rotary(rmsnorm(k.reshape(shp), k_norm), pos)
    v = v.reshape(shp)
    gates = jax.nn.sigmoid((gates + b_gate).astype(jnp.float32)).astype(h.dtype)
    return u, q, k, v, gates


def conv_branch(conv_in, conv_w, conv_b, ln_g, ln_b, w_conv_out):
    y = lax.conv_general_dilated(conv_in, conv_w[:, None, :], window_strides=(1,), padding="VALID",
                                 dimension_numbers=("NWC", "WIO", "NWC"), feature_group_count=CONV_CH) + conv_b
    yf = y.astype(jnp.float32)
    mu = jnp.mean(yf, axis=-1, keepdims=True)
    var = jnp.mean(jnp.square(yf - mu), axis=-1, keepdims=True)
    yn = ((yf - mu) * lax.rsqrt(var + NORM_EPS) * ln_g.astype(jnp.float32) + ln_b.astype(jnp.float32)).astype(y.dtype)
    return jax.nn.silu(yn) @ w_conv_out


def _with_prev_block(xb):
    prev = jnp.concatenate([jnp.zeros_like(xb[:, :1]), xb[:, :-1]], axis=1)
    return jnp.concatenate([prev, xb], axis=2)


def _band_scores(q, k, window, dil):
    b, tp, h, e = q.shape
    blk = window // dil
    nb = tp // window
    qb = q.reshape(b, nb, blk, dil, h, e)
    kb = _with_prev_block(k.reshape(b, nb, blk, dil, h, e))
    s = jnp.einsum("bnidhe,bnjdhe->bnidhj", qb, kb).astype(jnp.float32) * ATTN_SCALE
    i = jnp.arange(blk)[:, None]
    j = jnp.arange(2 * blk)[None, :]
    n = jnp.arange(nb)[:, None, None]
    rel = i + blk - j
    valid = (rel >= 0) & (rel <= blk) & (n * blk + j - blk >= 0)
    s = jnp.where(valid[None, :, :, None, None, :], s, -jnp.inf)
    return s.reshape(b, tp, h, 2 * blk)


def _band_values(p, v, window, dil):
    b, tp, h, e = v.shape
    blk = window // dil
    nb = tp // window
    vb = _with_prev_block(v.reshape(b, nb, blk, dil, h, e))
    pb = p.reshape(b, nb, blk, dil, h, 2 * blk).astype(v.dtype)
    return jnp.einsum("bnidhj,bnjdhe->bnidhe", pb, vb).reshape(b, tp, h, e)


def dilated_attention_prompt(q, k, v):
    b, t = q.shape[:2]
    w_max = max(WINDOWS)
    tp = -(-t // w_max) * w_max
    pad = ((0, 0), (0, tp - t), (0, 0), (0, 0))
    q, k, v = jnp.pad(q, pad), jnp.pad(k, pad), jnp.pad(v, pad)
    groups = list(zip(WINDOWS, DILATIONS))
    scores = [_band_scores(q[:, :, g * HEADS_PER_GROUP:(g + 1) * HEADS_PER_GROUP],
                           k[:, :, g * HEADS_PER_GROUP:(g + 1) * HEADS_PER_GROUP], w, d)
              for g, (w, d) in enumerate(groups)]
    p = jax.nn.softmax(jnp.concatenate(scores, axis=-1), axis=-1)
    outs = []
    off = 0
    for g, (w, d) in enumerate(groups):
        nk = scores[g].shape[-1]
        outs.append(_band_values(p[..., off:off + nk], v[:, :, g * HEADS_PER_GROUP:(g + 1) * HEADS_PER_GROUP], w, d))
        off += nk
    out = functools.reduce(jnp.add, outs)
    return out[:, :t].reshape(b, t, ATTN_OUT_WIDTH)


def dilated_attention_sample(q, k_full, v_full):
    b, s_len = q.shape[:2]
    scores, vals = [], []
    for g, (w, d) in enumerate(zip(WINDOWS, DILATIONS)):
        kf, vf = k_full[g], v_full[g]
        past = kf.shape[1] - s_len
        nk = w // d + 1
        idx = past + jnp.arange(s_len)[:, None] - d * jnp.arange(nk)[None, :]
        valid = idx >= 0
        idxc = jnp.maximum(idx, 0)
        kg = jnp.take(kf, idxc, axis=1)
        vals.append(jnp.take(vf, idxc, axis=1))
        sc = jnp.einsum("bshe,bsjhe->bshj", q[:, :, g * HEADS_PER_GROUP:(g + 1) * HEADS_PER_GROUP], kg)
        sc = sc.astype(jnp.float32) * ATTN_SCALE
        scores.append(jnp.where(valid[None, :, None, :], sc, -jnp.inf))
    p = jax.nn.softmax(jnp.concatenate(scores, axis=-1), axis=-1)
    outs = []
    off = 0
    for g in range(N_GROUPS):
        nk = scores[g].shape[-1]
        outs.append(jnp.einsum("bshj,bsjhe->bshe", p[..., off:off + nk].astype(vals[g].dtype), vals[g]))
        off += nk
    out = functools.reduce(jnp.add, outs)
    return out.reshape(b, s_len, ATTN_OUT_WIDTH)


def merge_branches(a, b_attn, gates, w_attn_out, w_out):
    g_a, g_b = jnp.split(gates, 2, axis=-1)
    return (g_a * a + g_b * (b_attn @ w_attn_out)) @ w_out


def setup_inputs(seed: int = 0) -> dict:
    key = jax.random.key(seed)
    ks = iter(jax.random.split(key, 40))
    nrm = lambda shape, scale: scale * jax.random.normal(next(ks), shape, jnp.float32)
    gain = lambda shape: 1.0 + nrm(shape, 0.02)
    L = DEPTH
    lens = [min(w, PAST_LEN) for w in WINDOWS]
    cshape = lambda n: (L, DEC_BATCH, n, HEADS_PER_GROUP, HEAD_DIM)
    return {
        "x_prompt": nrm((BATCH, SEQ, D_MODEL), 1.0),
        "x_sample": nrm((DEC_BATCH, DEC_SEQ, D_MODEL), 1.0),
        "cache_k_w128": nrm(cshape(lens[0]), 1.0),
        "cache_v_w128": nrm(cshape(lens[0]), 1.0),
        "cache_k_w512": nrm(cshape(lens[1]), 1.0),
        "cache_v_w512": nrm(cshape(lens[1]), 1.0),
        "cache_k_w2048": nrm(cshape(lens[2]), 1.0),
        "cache_v_w2048": nrm(cshape(lens[2]), 1.0),
        "state_conv": nrm((L, DEC_BATCH, CONV_WIDTH - 1, CONV_CH), 0.5),
        "ffn1_norm": gain((L, D_MODEL)),
        "ffn1_w_in": nrm((L, D_MODEL, 2 * D_FF), D_MODEL ** -0.5),
        "ffn1_w_out": nrm((L, D_FF, D_MODEL), D_FF ** -0.5),
        "mix_norm": gain((L, D_MODEL)),
        "w_in": nrm((L, D_MODEL, IN_WIDTH), D_MODEL ** -0.5),
        "b_gate": nrm((L, 2 * D_MODEL), 0.02),
        "q_norm": gain((L, N_HEADS, HEAD_DIM)),
        "k_norm": gain((L, N_HEADS, HEAD_DIM)),
        "conv_w": nrm((L, CONV_WIDTH, CONV_CH), CONV_WIDTH ** -0.5),
        "conv_b": nrm((L, CONV_CH), 0.02),
        "conv_ln_g": gain((L, CONV_CH)),
        "conv_ln_b": nrm((L, CONV_CH), 0.02),
        "w_conv_out": nrm((L, CONV_CH, D_MODEL), CONV_CH ** -0.5),
        "w_attn_out": nrm((L, ATTN_OUT_WIDTH, D_MODEL), ATTN_OUT_WIDTH ** -0.5),
        "w_out": nrm((L, D_MODEL, D_MODEL), D_MODEL ** -0.5),
        "ffn2_norm": gain((L, D_MODEL)),
        "ffn2_w_in": nrm((L, D_MODEL, 2 * D_FF), D_MODEL ** -0.5),
        "ffn2_w_out": nrm((L, D_FF, D_MODEL), D_FF ** -0.5),
    }


def reference(x_prompt, x_sample, cache_k_w128, cache_v_w128, cache_k_w512, cache_v_w512, cache_k_w2048, cache_v_w2048,
              state_conv, ffn1_norm, ffn1_w_in, ffn1_w_out, mix_norm, w_in, b_gate, q_norm, k_norm, conv_w, conv_b,
              conv_ln_g, conv_ln_b, w_conv_out, w_attn_out, w_out, ffn2_norm, ffn2_w_in, ffn2_w_out):
    t = x_prompt.shape[1]
    s_len = x_sample.shape[1]
    pos_p = jnp.arange(t)
    pos_s = PAST_LEN + jnp.arange(s_len)
    cache_k = (cache_k_w128, cache_k_w512, cache_k_w2048)
    cache_v = (cache_v_w128, cache_v_w512, cache_v_w2048)
    new_k_p = [[] for _ in range(N_GROUPS)]
    new_v_p = [[] for _ in range(N_GROUPS)]
    new_k_s = [[] for _ in range(N_GROUPS)]
    new_v_s = [[] for _ in range(N_GROUPS)]
    new_conv_p, new_conv_s = [], []
    xp, xs = x_prompt, x_sample
    for l in range(DEPTH):
        xp = xp + 0.5 * swiglu_ffn(rmsnorm(xp, ffn1_norm[l]), ffn1_w_in[l], ffn1_w_out[l])
        xs = xs + 0.5 * swiglu_ffn(rmsnorm(xs, ffn1_norm[l]), ffn1_w_in[l], ffn1_w_out[l])
        hp = rmsnorm(xp, mix_norm[l])
        hs = rmsnorm(xs, mix_norm[l])
        up, qp, kp, vp, gp = mix_project(hp, pos_p, w_in[l], b_gate[l], q_norm[l], k_norm[l])
        us, qs, ks, vs, gs = mix_project(hs, pos_s, w_in[l], b_gate[l], q_norm[l], k_norm[l])
        conv_in_p = jnp.concatenate([jnp.zeros((up.shape[0], CONV_WIDTH - 1, CONV_CH), up.dtype), up], axis=1)
        conv_in_s = jnp.concatenate([state_conv[l].astype(us.dtype), us], axis=1)
        a_p = conv_branch(conv_in_p, conv_w[l], conv_b[l], conv_ln_g[l], conv_ln_b[l], w_conv_out[l])
        a_s = conv_branch(conv_in_s, conv_w[l], conv_b[l], conv_ln_g[l], conv_ln_b[l], w_conv_out[l])
        new_conv_p.append(conv_in_p[:, -(CONV_WIDTH - 1):])
        new_conv_s.append(conv_in_s[:, -(CONV_WIDTH - 1):])
        b_p = dilated_attention_prompt(qp, kp, vp)
        k_full, v_full = [], []
        for g, w in enumerate(WINDOWS):
            hsl = slice(g * HEADS_PER_GROUP, (g + 1) * HEADS_PER_GROUP)
            kf = jnp.concatenate([cache_k[g][l].astype(ks.dtype), ks[:, :, hsl]], axis=1)
            vf = jnp.concatenate([cache_v[g][l].astype(vs.dtype), vs[:, :, hsl]], axis=1)
            k_full.append(kf)
            v_full.append(vf)
            keep_p = min(w, t)
            keep_s = min(w, PAST_LEN + s_len)
            new_k_p[g].append(kp[:, t - keep_p:, hsl])
            new_v_p[g].append(vp[:, t - keep_p:, hsl])
            new_k_s[g].append(kf[:, kf.shape[1] - keep_s:])
            new_v_s[g].append(vf[:, vf.shape[1] - keep_s:])
        b_s = dilated_attention_sample(qs, k_full, v_full)
        xp = xp + merge_branches(a_p, b_p, gp, w_attn_out[l], w_out[l])
        xs = xs + merge_branches(a_s, b_s, gs, w_attn_out[l], w_out[l])
        xp = xp + 0.5 * swiglu_ffn(rmsnorm(xp, ffn2_norm[l]), ffn2_w_in[l], ffn2_w_out[l])
        xs = xs + 0.5 * swiglu_ffn(rmsnorm(xs, ffn2_norm[l]), ffn2_w_in[l], ffn2_w_out[l])
    return (xp, xs,
            jnp.stack(new_k_p[0]), jnp.stack(new_v_p[0]), jnp.stack(new_k_p[1]), jnp.stack(new_v_p[1]),
            jnp.stack(new_k_p[2]), jnp.stack(new_v_p[2]), jnp.stack(new_conv_p),
            jnp.stack(new_k_s[0]), jnp.stack(new_v_s[0]), jnp.stack(new_k_s[1]), jnp.stack(new_v_s[1]),
            jnp.stack(new_k_s[2]), jnp.stack(new_v_s[2]), jnp.stack(new_conv_s))
```

```python
import contextlib
import numpy as np
import concourse.bass as bass
import concourse.mybir as mybir
from concourse.bass_utils import run_bass_kernel_spmd

F32 = mybir.dt.float32
BF16 = mybir.dt.bfloat16
ALU = mybir.AluOpType
AF = mybir.ActivationFunctionType
AX = mybir.AxisListType

COMPUTE = ("pe", "act", "dve", "pool")
D = 1024
DFF = 2816
T = 512
EPS = 1e-6
NCORES = 8


class Op:
    __slots__ = ("eng", "fn", "deps", "dma", "cls", "slot", "use", "waits", "ordinal", "signal", "vc", "bulk")

    def __init__(self, eng, fn, deps, dma, cls):
        self.eng = eng
        self.fn = fn
        self.deps = deps
        self.dma = dma
        self.cls = cls
        self.slot = None
        self.use = 0
        self.waits = []
        self.ordinal = 0
        self.signal = False
        self.vc = None
        self.bulk = False


class Prog:
    def __init__(self, nc, dma_classes=None):
        self.nc = nc
        self.ops = []
        self.lw = {}
        self.rd = {}
        self.dma_classes = dma_classes or {"ld": 10, "st": 6, "bk": 3}

    def add(self, eng, fn, reads=(), writes=(), dma=False, cls="ld", bulk=False):
        idx = len(self.ops)
        deps = {}
        extra = [t for t in reads if isinstance(t, tuple) and t[0] == "ps" and t not in writes]
        if extra:
            writes = list(writes) + extra
        for t in reads:
            w = self.lw.get(t)
            if w is not None:
                deps[w] = True
        for t in writes:
            w = self.lw.get(t)
            if w is not None:
                deps.setdefault(w, False)
            r = self.rd.get(t)
            if r:
                for k in r.values():
                    if isinstance(k, list):
                        for kk in k:
                            deps.setdefault(kk, False)
                    else:
                        deps.setdefault(k, False)
        for t in reads:
            r = self.rd.setdefault(t, {})
            if dma:
                r.setdefault("dma", []).append(idx)
            else:
                r[eng] = idx
        for t in writes:
            self.lw[t] = idx
            self.rd[t] = {}
        op = Op(eng, fn, deps, dma, cls)
        op.bulk = bulk
        self.ops.append(op)
        return idx

    def emit(self, final_waits_engine="sp"):
        nc = self.nc
        ops = self.ops
        slot_names = []
        cls_slots = {}
        for c, n in self.dma_classes.items():
            cls_slots[c] = [len(slot_names) + i for i in range(n)]
            slot_names += [f"{c}{i}" for i in range(n)]
        cls_ptr = {c: 0 for c in self.dma_classes}
        slot_last = {}
        slot_uses = {s: 0 for s in range(len(slot_names))}
        bulk_count = 0
        for i, op in enumerate(ops):
            if not op.dma:
                continue
            if op.bulk:
                bulk_count += 1
                continue
            sl = cls_slots[op.cls]
            s = sl[cls_ptr[op.cls] % len(sl)]
            cls_ptr[op.cls] += 1
            op.slot = s
            slot_uses[s] += 1
            op.use = slot_uses[s]
            if s in slot_last:
                op.deps.setdefault(slot_last[s], False)
            slot_last[s] = i
        eng_count = {e: 0 for e in COMPUTE}
        eng_vc = {e: {} for e in ("pe", "act", "dve", "pool", "sp")}
        waited = set()
        for i, op in enumerate(ops):
            e = op.eng
            cur = eng_vc[e]
            need = {}
            depvcs = []
            for d, raw in op.deps.items():
                dop = ops[d]
                if dop.dma:
                    if dop.bulk:
                        continue
                    src = ("d", dop.slot)
                    val = dop.use
                else:
                    if dop.eng == e and not op.dma:
                        if e == "pe":
                            continue
                    src = dop.eng
                    val = dop.ordinal
                if val > need.get(src, 0):
                    need[src] = val
                depvcs.append(dop.vc)
            for src, val in need.items():
                if val > cur.get(src, 0):
                    op.waits.append((src, val))
                    waited.add((src, val))
                    cur[src] = val
            for v in depvcs:
                for src, val in v.items():
                    if src == e and e in COMPUTE:
                        continue
                    if val > cur.get(src, 0):
                        cur[src] = val
            if op.dma:
                v = dict(cur)
                if not op.bulk:
                    v[("d", op.slot)] = op.use
                op.vc = v
            else:
                eng_count[e] += 1
                op.ordinal = eng_count[e]
                v = dict(cur)
                v[e] = op.ordinal
                op.vc = v
        remap = {e: {} for e in COMPUTE}
        cnt = {e: 0 for e in COMPUTE}
        for op in ops:
            if op.dma:
                continue
            if (op.eng, op.ordinal) in waited:
                op.signal = True
                cnt[op.eng] += 1
                remap[op.eng][op.ordinal] = cnt[op.eng]
        self.stats = dict(n_ops=len(ops), signals=dict(cnt), n_waits=sum(len(o.waits) for o in ops),
                          per_eng={e: sum(1 for o in ops if o.eng == e) for e in ("pe", "act", "dve", "pool", "sp")})
        with contextlib.ExitStack() as st:
            esem = {e: st.enter_context(nc.semaphore(f"s_{e}")) for e in COMPUTE}
            dsem = [st.enter_context(nc.semaphore(f"s_{n}")) for n in slot_names]
            bsem = st.enter_context(nc.semaphore("s_bulk"))
            block = st.enter_context(nc.Block())
            streams = {e: [op for op in ops if op.eng == e] for e in ("pe", "act", "dve", "pool", "sp")}

            def run_stream(engh, e):
                for op in streams[e]:
                    for src, val in op.waits:
                        if isinstance(src, tuple):
                            engh.wait_ge(dsem[src[1]], 16 * val)
                        else:
                            engh.wait_ge(esem[src], remap[src][val])
                    inst = op.fn(engh)
                    if op.dma:
                        if op.bulk:
                            inst.then_inc(bsem, 16)
                        else:
                            inst.then_inc(dsem[op.slot], 16)
                    elif op.signal:
                        inst.then_inc(esem[e], 1)
                if e == final_waits_engine:
                    for s, n in slot_uses.items():
                        if n:
                            engh.wait_ge(dsem[s], 16 * n)
                    if bulk_count:
                        engh.wait_ge(bsem, 16 * bulk_count)

            @block.tensor
            def _(eng):
                run_stream(eng, "pe")

            @block.scalar
            def _(eng):
                run_stream(eng, "act")

            @block.vector
            def _(eng):
                run_stream(eng, "dve")

            @block.gpsimd
            def _(eng):
                run_stream(eng, "pool")

            @block.sync
            def _(eng):
                run_stream(eng, "sp")


C_F1N, C_MIXN, C_F2N, C_BG, C_CB, C_LNG, C_LNB, C_CW = 0, 8, 16, 24, 40, 44, 48, 52
NCOLS = 52 + 124


def build(nc, n_halo=4, n_main=4, do_sample=True):
    import os
    STOP = int(os.environ.get("KSTOP", "99"))
    NTP = (n_halo + n_main) * T
    dt_in = lambda n, s: nc.dram_tensor(n, list(s), F32, kind="ExternalInput").ap()
    dt_out = lambda n, s: nc.dram_tensor(n, list(s), F32, kind="ExternalOutput").ap()
    xp = dt_in("xp", (NTP, D))
    xs = dt_in("xs", (64, D))
    f1wi = dt_in("f1wi", (D, 2 * DFF))
    f1wo = dt_in("f1wo", (DFF, D))
    win = dt_in("win", (D, 5376))
    wco = dt_in("wco", (512, D))
    wao = dt_in("wao", (256, D))
    wo = dt_in("wo", (D, D))
    f2wi = dt_in("f2wi", (D, 2 * DFF))
    f2wo = dt_in("f2wo", (DFF, D))
    cols_d = dt_in("cols", (128, NCOLS))
    qkg_d = dt_in("qkg", (128, 1536))
    ropep_d = dt_in("ropep", (NTP, 16))
    ropes_d = dt_in("ropes", (64, 16))
    masks_d = dt_in("masks", (128, 512))
    smask_d = dt_in("smask", (128, 4 + 128))
    ident_d = dt_in("ident", (128, 128))
    ck = [dt_in(f"ck{w}", (16, w, 256)) for w in (128, 512, 2048)]
    cv = [dt_in(f"cv{w}", (16, w, 256)) for w in (128, 512, 2048)]
    sconv = dt_in("sconv", (16, 30, 512))
    NM = n_main * T
    y_p = dt_out("y_p", (NM, D))
    y_s = dt_out("y_s", (64, D))
    okp = [dt_out(f"okp{g}", (NM, 256)) for g in range(3)]
    ovp = [dt_out(f"ovp{g}", (NM, 256)) for g in range(3)]
    oconv_p = dt_out("oconv_p", (30, 512))
    oks = [dt_out(f"oks{w}", (16, w, 256)) for w in (128, 512, 2048)]
    ovs = [dt_out(f"ovs{w}", (16, w, 256)) for w in (128, 512, 2048)]
    oconv_s = dt_out("oconv_s", (16, 30, 512))

    P = Prog(nc)
    st = contextlib.ExitStack()
    sb = lambda n, s, d: st.enter_context(nc.sbuf_tensor("sb_" + n, list(s), d))
    xin = sb("xin", (128, 2, 1024), F32)
    xT = sb("xT", (128, 8, T), F32)
    hT = sb("hT", (128, 8, T), BF16)
    R = sb("R", (128, 22, T), BF16)
    NSTG, NWP, PCAP = 2, 5, 2048
    stg = sb("stg", (128, NSTG, PCAP), F32)
    wp = sb("wp", (128, NWP, PCAP), BF16)
    rstd = sb("rstd", (128, T), F32)
    sg = sb("sg", (128, 2, T), F32)
    uT = sb("uT", (128, 4, 32 + T), F32)
    uTb = sb("uTb", (128, 4, 576), BF16)
    KTc = uTb[:, :, :].rearrange("p c n -> p (c n)").rearrange("p (a k n) -> p a k n", a=2, n=128)
    dg = sb("dg", (128, 8, 128), BF16)
    utail = sb("utail", (128, 4, 30), F32)
    KTs = sb("KTs", (128, 6, 64), BF16)
    VS = sb("VS", (64, 3, 256), BF16)
    PTs = sb("PTs", (128, 192), BF16)
    cacc = sb("cacc", (128, 4, T), F32)
    sT = sb("sT", (128, 4, T), BF16)
    oT = sb("oT", (64, 4, T), BF16)
    KT = [sb("KT0", (128, 2, 2 * T), BF16), sb("KT1", (128, 2, 2 * T), BF16), sb("KT2", (128, 2, 8 * T), BF16)]
    VH = [sb("V0", (128, 8, 256), BF16), sb("V1", (128, 8, 256), BF16), sb("V2", (128, 32, 256), BF16)]
    PT = sb("PT", (128, 1024), BF16)
    qn = sb("qn", (128, 4, 256), F32)
    sqt = sb("sqt", (128, 4, 256), F32)
    qb = sb("qb", (128, 4, 256), BF16)
    rt = sb("rt", (128, 4, 128), F32)
    ssq = sb("ssq", (128, 16), F32)
    vf = sb("vf", (128, 2, 256), F32)
    ident_f = sb("ident_f", (128, 128), F32)
    ident_b = sb("ident_b", (128, 128), BF16)
    ones_b = sb("ones_b", (128, 128), BF16)
    ones_f = sb("ones_f", (128, 128), F32)
    masks = sb("masks", (128, 512), BF16)
    smask = sb("smask", (128, 132), F32)
    cols = sb("cols", (128, NCOLS), F32)
    qkg = sb("qkg", (128, 1536), F32)
    rope = sb("rope", (128, 4, 16), F32)
    epsc = sb("epsc", (128, 1), F32)
    ps = st.enter_context(nc.psum_tensor("ps", [128, 8, 512], F32))

    state = {"psb": 0, "stg": 0, "wp": 0, "fresh": [False] * 8, "sg": 0, "vf": 0}

    def ps_alloc(n=1):
        b = state["psb"]
        lo = state.get("ps_lo", 0)
        if b + n > 8 or b < lo:
            b = lo
        state["psb"] = (b + n) % 8
        for i in range(n):
            state["fresh"][b + i] = True
        return b

    def pst(b, n=1):
        return [("ps", b + i) for i in range(n)]

    def mm(out, lhsT, rhs, bank, reads, extra_writes=(), tp=None):
        first = state["fresh"][bank]
        state["fresh"][bank] = False
        kw = dict(start=first, stop=True, skip_group_check=True)
        if tp is not None:
            kw["tile_position"] = tp
        P.add("pe", lambda e: e.matmul(out, lhsT=lhsT, rhs=rhs, **kw), reads=list(reads) + [("ps", bank)],
              writes=[("ps", bank)] + list(extra_writes))

    def tr(out, in_, ident, bank, reads):
        state["fresh"][bank] = False
        P.add("pe", lambda e: e.transpose(out, in_, ident), reads=list(reads) + [("ps", bank)], writes=[("ps", bank)])

    def dma(eng, out, in_, reads=(), writes=(), cls="ld", bulk=False):
        P.add(eng, lambda e: e.dma_start(out=out, in_=in_), reads=reads, writes=writes, dma=True, cls=cls, bulk=bulk)

    NPAN = 128
    wscr = nc.dram_tensor("wscr", [NPAN, 128, PCAP], BF16).ap()
    pan_seen = {}
    pan_pending = []
    cast_rot = ["dve", "act"]

    def cast_op(eng, out, in_, reads, writes):
        if eng == "act":
            P.add("act", lambda e: e.activation(out=out, in_=in_, func=AF.Copy), reads=reads, writes=writes)
        else:
            P.add(eng, lambda e: e.tensor_copy(out=out, in_=in_), reads=reads, writes=writes)

    def flush_pending(keep):
        while len(pan_pending) > keep:
            idx, w, np_, n = pan_pending.pop(0)
            dma("sp", wscr[idx, 0:np_, 0:n], wp[0:np_, w, 0:n], reads=[("wp", w)], writes=[("scr", idx)], cls="st")

    def take_wp():
        w = state["wp"]
        state["wp"] = (w + 1) % NWP
        if any(p_[1] == w for p_ in pan_pending):
            flush_pending(0)
        return w

    def panel(src_ap, shape, key=None):
        np_ = shape[0]
        n = int(np.prod(shape[1:]))
        w = take_wp()
        names = "abcd"[: len(shape) - 1]
        pat = "p (" + " ".join(names) + ") -> p " + " ".join(names)
        kws = {names[i]: shape[i + 1] for i in range(len(names))}
        wview = wp[0:np_, w, 0:n].rearrange(pat, **kws) if len(shape) > 2 else wp[0:np_, w, 0:n]
        if key is not None and key in pan_seen:
            idx = pan_seen[key]
            if any(p_[0] == idx for p_ in pan_pending):
                flush_pending(0)
            dma("sp", wp[0:np_, w, 0:n], wscr[idx, 0:np_, 0:n], reads=[("scr", idx)], writes=[("wp", w)])
            return wview, ("wp", w), None, None
        s = state["stg"]
        state["stg"] = (s + 1) % NSTG
        sview = stg[0:np_, s, 0:n].rearrange(pat, **kws) if len(shape) > 2 else stg[0:np_, s, 0:n]
        if len(shape) == 4:
            for i2 in range(shape[2]):
                dma("sp", sview[:, :, i2, :], src_ap[:, :, i2, :], reads=[("stg", s)] if i2 else [], writes=[("stg", s)])
        else:
            dma("sp", sview, src_ap, writes=[("stg", s)])
        eng = cast_rot[state.get("castc", 0) % len(cast_rot)]
        state["castc"] = state.get("castc", 0) + 1
        cast_op(eng, wp[0:np_, w, 0:n], stg[0:np_, s, 0:n], [("stg", s)], [("wp", w)])
        if key is not None:
            idx = len(pan_seen)
            assert idx < NPAN
            pan_seen[key] = idx
            pan_pending.append((idx, w, np_, n))
            flush_pending(NSTG)
        return wview, ("wp", w), sview, ("stg", s)

    dma("sp", cols[:], cols_d, writes=["cols"])
    dma("sp", qkg[:], qkg_d, writes=["qkg"])
    dma("sp", sg[:, 0, :], masks_d, writes=[("sg", 0)])
    dma("sp", smask[:], smask_d, writes=["smask"])
    dma("sp", ident_f[:], ident_d, writes=["ident_f"])
    P.add("dve", lambda e: e.tensor_copy(out=ident_b[:], in_=ident_f[:]), reads=["ident_f"], writes=["ident_b"])
    P.add("dve", lambda e: e.tensor_copy(out=masks[:], in_=sg[:, 0, :]), reads=[("sg", 0)], writes=["masks"])
    P.add("dve", lambda e: e.memset(ones_b[:], 1.0), writes=["ones_b"])
    P.add("dve", lambda e: e.memset(ones_f[:], 1.0), writes=["ones_f"])
    P.add("dve", lambda e: e.memset(epsc[:], EPS), writes=["epsc"])
    P.add("pool", lambda e: e.memset(KT[2][:], 0.0), writes=[("KT", 2, s_) for s_ in range(8)])
    P.add("pool", lambda e: e.memset(VH[2][:], 0.0), writes=[("V", 2, b_) for b_ in range(32)])
    P.add("pool", lambda e: e.memset(uTb[:], 0.0), writes=["uTb"])
    bulk_chunks = []
    if do_sample:
        for b0 in range(16):
            for (o_t, i_t) in ((oks[2], ck[2]), (ovs[2], cv[2])):
                bulk_chunks.append((o_t[b0:b0 + 1, 0:1022, :], i_t[b0:b0 + 1, 4:1026, :]))
                bulk_chunks.append((o_t[b0:b0 + 1, 1022:2044, :], i_t[b0:b0 + 1, 1026:2048, :]))
        for b0 in range(0, 16, 2):
            bulk_chunks.append((oks[1][b0:b0 + 2, 0:508, :], ck[1][b0:b0 + 2, 4:512, :]))
            bulk_chunks.append((ovs[1][b0:b0 + 2, 0:508, :], cv[1][b0:b0 + 2, 4:512, :]))
        for b0 in range(0, 16, 8):
            bulk_chunks.append((oks[0][b0:b0 + 8, 0:124, :], ck[0][b0:b0 + 8, 4:128, :]))
            bulk_chunks.append((ovs[0][b0:b0 + 8, 0:124, :], cv[0][b0:b0 + 8, 4:128, :]))
        bulk_chunks.append((oconv_s[:, 0:26, :], sconv[:, 4:30, :]))

    def drip(n=1):
        for _ in range(n):
            if bulk_chunks:
                o_, i_ = bulk_chunks.pop(0)
                dma("sp", o_, i_, cls="bk")

    def load_x(src, ntok, N):
        nb = (ntok + 127) // 128
        for tb in range(nb):
            rows = min(128, ntok - tb * 128)
            sl = tb % 2
            dma("sp", xin[0:rows, sl, :], src[tb * 128: tb * 128 + rows, :], writes=[("xin", sl)])
            for half in range(2):
                b = ps_alloc()
                for cc in range(4):
                    c = half * 4 + cc
                    tr(ps[:, b, cc * 128: cc * 128 + rows], xin[0:rows, sl, c * 128:(c + 1) * 128], ident_f[0:rows, 0:rows], b,
                       [("xin", sl), "ident_f"])
                o = xT[:, half * 4:(half + 1) * 4, tb * 128: tb * 128 + rows]
                i_ = ps[:, b, :].rearrange("p (c n) -> p c n", c=4)[:, :, 0:rows]
                eng = "act" if half == 0 else "dve"
                if eng == "act":
                    P.add("act", lambda e, o=o, i_=i_: e.activation(out=o, in_=i_, func=AF.Copy), reads=pst(b),
                          writes=[("xT", c_) for c_ in range(half * 4, half * 4 + 4)])
                else:
                    P.add("dve", lambda e, o=o, i_=i_: e.tensor_copy(out=o, in_=i_), reads=pst(b),
                          writes=[("xT", c_) for c_ in range(half * 4, half * 4 + 4)])

    def store_x(dst, ntok):
        nb = (ntok + 127) // 128
        for tb in range(nb):
            rows = min(128, ntok - tb * 128)
            sl = tb % 2
            for half in range(2):
                b = ps_alloc()
                for cc in range(4):
                    c = half * 4 + cc
                    tr(ps[0:rows, b, cc * 128:(cc + 1) * 128], xT[:, c, tb * 128: tb * 128 + rows], ident_f[:], b,
                       [("xT", c), "ident_f"])
                o = xin[0:rows, sl, half * 512:(half + 1) * 512]
                i_ = ps[0:rows, b, :]
                if half == 0:
                    P.add("act", lambda e, o=o, i_=i_: e.activation(out=o, in_=i_, func=AF.Copy), reads=pst(b), writes=[("xin", sl)])
                else:
                    P.add("dve", lambda e, o=o, i_=i_: e.tensor_copy(out=o, in_=i_), reads=pst(b) + [("xin", sl)], writes=[("xin", sl)])
            dma("act", dst[tb * 128: tb * 128 + rows, :], xin[0:rows, sl, :], reads=[("xin", sl)], cls="st")

    def rmsnorm(N, gcol, dst, dst_tok):
        P.add("act", lambda e: e.activation(out=dst[:, :, 0:N], in_=xT[:, :, 0:N], func=AF.Square),
              reads=[("xT", c) for c in range(8)], writes=[(dst_tok, c) for c in range(8)])
        b = ps_alloc()
        for c in range(8):
            mm(ps[:, b, 0:N], ones_b[:], dst[:, c, 0:N], b, ["ones_b", (dst_tok, c)])
        P.add("act", lambda e: e.activation(out=rstd[:, 0:N], in_=ps[:, b, 0:N], func=AF.Sqrt, bias=epsc[:, 0:1], scale=1.0 / D),
              reads=pst(b) + ["epsc"], writes=["rstd"])
        P.add("dve", lambda e: e.reciprocal(out=rstd[:, 0:N], in_=rstd[:, 0:N]), reads=["rstd"], writes=["rstd"])
        for c in range(8):
            eng = "dve"
            P.add(eng, lambda e, c=c: e.scalar_tensor_tensor(out=dst[:, c, 0:N], in0=xT[:, c, 0:N], scalar=cols[:, gcol + c: gcol + c + 1],
                                                          in1=rstd[:, 0:N], op0=ALU.mult, op1=ALU.mult),
                  reads=[("xT", c), "cols", "rstd"], writes=[(dst_tok, c)])

    def ffn(N, gcol, wi, wo_):
        rmsnorm(N, gcol, hT, "hT")
        wi_v = wi.rearrange("(k p) (two j c) -> p k two j c", p=128, two=2, c=128)
        for j in range(22):
            if j % 3 == 0:
                drip()
            wv, wt, _, _ = panel(wi_v[:, :, :, j, :], (128, 8, 2, 128), key=("fi", gcol, j))
            bg = ps_alloc()
            for k in range(8):
                mm(ps[:, bg, 0:N], wv[:, k, 0, :], hT[:, k, 0:N], bg, [wt, ("hT", k)])
            bu = ps_alloc()
            for k in range(8):
                mm(ps[:, bu, 0:N], wv[:, k, 1, :], hT[:, k, 0:N], bu, [wt, ("hT", k)])
            s_ = state["sg"]
            state["sg"] = 1 - s_
            P.add("act", lambda e, bg=bg, s_=s_: e.activation(out=sg[:, s_, 0:N], in_=ps[:, bg, 0:N], func=AF.Silu),
                  reads=pst(bg), writes=[("sg", s_)])
            P.add("dve", lambda e, bu=bu, s_=s_, j=j: e.tensor_tensor(out=R[:, j, 0:N], in0=sg[:, s_, 0:N], in1=ps[:, bu, 0:N], op=ALU.mult),
                  reads=pst(bu) + [("sg", s_)], writes=[("R", j)])
        wo_v = wo_.rearrange("(j p) (c n) -> p j c n", p=128, n=128)
        for c in range(8):
            b = ps_alloc()
            for hj in range(2):
                wv, wt, _, _ = panel(wo_v[:, hj * 11:(hj + 1) * 11, c, :], (128, 11, 128), key=("fo", gcol, c, hj))
                for jj in range(11):
                    j = hj * 11 + jj
                    mm(ps[:, b, 0:N], wv[:, jj, :], R[:, j, 0:N], b, [wt, ("R", j)])
            P.add("dve", lambda e, b=b, c=c: e.scalar_tensor_tensor(out=xT[:, c, 0:N], in0=ps[:, b, 0:N], scalar=0.5, in1=xT[:, c, 0:N],
                                                                 op0=ALU.mult, op1=ALU.add),
                  reads=pst(b) + [("xT", c)], writes=[("xT", c)])

    win_k = win.rearrange("(k p) n -> p k n", p=128)

    def qk_A(N, nb, col0):
        wv, wt, _, _ = panel(win_k[:, :, col0: col0 + 256], (128, 8, 256), key=("qk", col0))
        b2 = ps_alloc(2)
        for tb in range(nb):
            rows = min(128, N - tb * 128)
            bank = b2 + tb // 2
            off = (tb % 2) * 256
            for k in range(8):
                mm(ps[0:rows, bank, off: off + 256], hT[:, k, tb * 128: tb * 128 + rows], wv[:, k, :], bank, [wt, ("hT", k)])
        return b2

    def qk_panel(N, nb, col0, gain0, rope_t, dstT, dst_tok_fn, kout=None):
        qk_B(qk_A(N, nb, col0), N, nb, gain0, rope_t, dstT, kout)

    def qk_B(b2, N, nb, gain0, rope_t, dstT, kout=None):
        rows = min(128, N)
        pq = ps[0:rows, b2:b2 + 2, :].rearrange("p b (t n) -> p (b t) n", t=2)[:, 0:nb, :]
        P.add("act", lambda e: e.activation(out=sqt[0:rows, 0:nb, :], in_=pq, func=AF.Square), reads=pst(b2, 2), writes=[("sqt", 0), ("sqt", 1)])
        P.add("dve", lambda e: e.tensor_reduce(out=ssq[0:rows, 0:nb * 4], in_=sqt[0:rows, 0:nb, :].rearrange("p t (h e) -> p (t h) e", e=64),
                                               axis=AX.X, op=ALU.add), reads=[("sqt", 0), ("sqt", 1)], writes=["ssq"])
        P.add("act", lambda e: e.activation(out=ssq[0:rows, 0:nb * 4], in_=ssq[0:rows, 0:nb * 4], func=AF.Sqrt, bias=epsc[0:rows, 0:1], scale=1.0 / 64),
              reads=["ssq", "epsc"], writes=["ssq"])
        P.add("dve", lambda e: e.reciprocal(out=ssq[0:rows, 0:nb * 4], in_=ssq[0:rows, 0:nb * 4]), reads=["ssq"], writes=["ssq"])
        P.add("dve", lambda e: e.tensor_tensor(out=qn[0:rows, 0:nb, :].rearrange("p t (h e) -> p (t h) e", e=64),
                                               in0=pq.rearrange("p t (h e) -> p (t h) e", e=64),
                                               in1=ssq[0:rows, 0:nb * 4].unsqueeze(2).to_broadcast([rows, nb * 4, 64]), op=ALU.mult),
              reads=pst(b2, 2) + ["ssq"], writes=["qn"])
        P.add("dve", lambda e: e.tensor_tensor(out=qn[0:rows, 0:nb, :], in0=qn[0:rows, 0:nb, :],
                                                in1=qkg[0:rows, gain0: gain0 + 256].unsqueeze(1).to_broadcast([rows, nb, 256]), op=ALU.mult),
              reads=["qn", "qkg"], writes=["qn"])
        q4 = qn[0:rows, 0:nb, :].rearrange("p t (h e) -> p t h e", e=64)
        x1, x2 = q4[:, :, :, 0:8], q4[:, :, :, 8:16]
        cs = rope_t[0:rows, 0:nb, 0:8].unsqueeze(2).to_broadcast([rows, nb, 4, 8])
        sn = rope_t[0:rows, 0:nb, 8:16].unsqueeze(2).to_broadcast([rows, nb, 4, 8])
        r4 = rt[0:rows, 0:nb, :].rearrange("p t (a h e) -> p t a h e", a=4, e=8)
        t1, t2, t3, t4 = r4[:, :, 0], r4[:, :, 1], r4[:, :, 2], r4[:, :, 3]
        P.add("dve", lambda e: e.tensor_tensor(out=t1, in0=x1, in1=cs, op=ALU.mult), reads=["qn", "rope"], writes=["rt"])
        P.add("dve", lambda e: e.tensor_tensor(out=t2, in0=x2, in1=sn, op=ALU.mult), reads=["qn", "rope"], writes=["rt"])
        P.add("dve", lambda e: e.tensor_tensor(out=t3, in0=x2, in1=cs, op=ALU.mult), reads=["qn", "rope"], writes=["rt"])
        P.add("dve", lambda e: e.tensor_tensor(out=t4, in0=x1, in1=sn, op=ALU.mult), reads=["qn", "rope"], writes=["rt"])
        P.add("dve", lambda e: e.tensor_tensor(out=x1, in0=t1, in1=t2, op=ALU.subtract), reads=["rt", "qn"], writes=["qn"])
        P.add("dve", lambda e: e.tensor_tensor(out=x2, in0=t3, in1=t4, op=ALU.add), reads=["rt", "qn"], writes=["qn"])
        P.add("act", lambda e: e.activation(out=qb[0:rows, 0:nb, :], in_=qn[0:rows, 0:nb, :], func=AF.Copy), reads=["qn"], writes=["qb"])
        if kout is not None:
            kout(qn)
        for ch in range(2):
            b = ps_alloc()
            pb = ps[:, b, :].bitcast(BF16)
            for tb in range(nb):
                rws = min(128, N - tb * 128)
                tr(pb[:, tb * 128: tb * 128 + rws], qb[0:rws, tb, ch * 128:(ch + 1) * 128], ident_b[0:rws, 0:rws], b, ["qb", "ident_b"])
            o, otoks = dstT(ch)
            eng = "act" if ch == 0 else "dve"
            if eng == "act":
                P.add("act", lambda e, o=o, pb=pb: e.activation(out=o, in_=pb[:, 0:N], func=AF.Copy), reads=pst(b), writes=otoks)
            else:
                P.add("dve", lambda e, o=o, pb=pb: e.tensor_copy(out=o, in_=pb[:, 0:N]), reads=pst(b), writes=otoks)

    def out_rows(dst, src, rows_tok, reads):
        dma("act", dst, src, reads=reads, cls="st")

    def prompt_tile(tau):
        is_main = tau >= n_halo
        q_idx = tau - n_halo
        p0 = tau * T
        load_x(xp[p0: p0 + T, :], T, T)
        if STOP <= 1:
            if is_main:
                store_x(y_p[q_idx * T:(q_idx + 1) * T, :], T)
            return
        ffn(T, C_F1N, f1wi, f1wo)
        if STOP <= 2:
            if is_main:
                store_x(y_p[q_idx * T:(q_idx + 1) * T, :], T)
            return
        rmsnorm(T, C_MIXN, hT, "hT")
        dma("sp", rope[:], ropep_d[p0: p0 + T, :].rearrange("(t p) c -> p t c", p=128), writes=["rope"])
        last_halo = (tau == n_halo - 1)
        if is_main or last_halo:
            P.add("act", lambda e: e.activation(out=uTb[:, :, 0:30], in_=uTb[:, :, T: T + 30], func=AF.Copy), reads=["uTb"], writes=["uTb"])
            for c in range(4):
                src = win_k[:, :, 0:1024].rearrange("p k (two j c) -> p k two j c", two=2, c=128)[:, :, :, c, :]
                wv, wt, _, _ = panel(src, (128, 8, 2, 128), key=("u", c))
                ba = ps_alloc()
                for k in range(8):
                    mm(ps[:, ba, :], wv[:, k, 0, :], hT[:, k, :], ba, [wt, ("hT", k)])
                bb = ps_alloc()
                for k in range(8):
                    mm(ps[:, bb, :], wv[:, k, 1, :], hT[:, k, :], bb, [wt, ("hT", k)])
                s_ = state["sg"]
                state["sg"] = 1 - s_
                P.add("act", lambda e, bb=bb, s_=s_: e.activation(out=sg[:, s_, :], in_=ps[:, bb, :], func=AF.Sigmoid), reads=pst(bb), writes=[("sg", s_)])
                P.add("dve", lambda e, ba=ba, s_=s_, c=c: e.tensor_tensor(out=uTb[:, c, 30: 30 + T], in0=sg[:, s_, :], in1=ps[:, ba, :], op=ALU.mult),
                      reads=pst(ba) + [("sg", s_)], writes=["uTb"])
                if tau == n_halo + n_main - 1:
                    P.add("dve", lambda e, ba=ba, s_=s_, c=c: e.tensor_tensor(out=utail[:, c, :], in0=sg[:, s_, T - 30: T], in1=ps[:, ba, T - 30: T], op=ALU.mult),
                          reads=pst(ba) + [("sg", s_)], writes=["utail"])
        if is_main:
            for m in range(8):
                src = win_k[:, :, 3328 + 256 * m: 3328 + 256 * (m + 1)].rearrange("p k (two c) -> p k two c", two=2)
                wv, wt, _, _ = panel(src, (128, 8, 2, 128), key=("g", m))
                for t2 in range(2):
                    c = 2 * m + t2
                    b = ps_alloc()
                    for k in range(8):
                        mm(ps[:, b, :], wv[:, k, t2, :], hT[:, k, :], b, [wt, ("hT", k)])
                    P.add("act", lambda e, b=b, c=c: e.activation(out=R[:, c, :], in_=ps[:, b, :], func=AF.Sigmoid, bias=cols[:, C_BG + c: C_BG + c + 1], scale=1.0),
                          reads=pst(b) + ["cols"], writes=[("R", c)])
        if STOP <= 3:
            if is_main:
                store_x(y_p[q_idx * T:(q_idx + 1) * T, :], T)
            return
        specs = []
        need_g = [0, 1, 2] if (is_main or last_halo) else [2]
        for g in need_g:
            slot = tau % (8 if g == 2 else 2)

            def kdst(ch, g=g, slot=slot):
                return KT[g][:, ch, slot * T:(slot + 1) * T], [("KT", g, slot)]

            def kout(qn_, g=g):
                if is_main:
                    dma("act", okp[g][q_idx * T:(q_idx + 1) * T, :].rearrange("(t p) c -> p t c", p=128), qn_[:, :, :], reads=["qn"], cls="st")

            specs.append((1792 + 256 * g, 768 + 256 * g, kdst, kout))
            if is_main:
                def qdst(ch, g=g):
                    return R[:, 16 + 2 * g + ch, :], [("R", 16 + 2 * g + ch)]
                specs.append((1024 + 256 * g, 256 * g, qdst, None))
        prev = None
        for (col0, gain0, dst_, kout_) in specs:
            b2_ = qk_A(T, 4, col0)
            if prev is not None:
                qk_B(prev[0], T, 4, prev[1], rope, prev[2], prev[3])
            prev = (b2_, gain0, dst_, kout_)
        qk_B(prev[0], T, 4, prev[1], rope, prev[2], prev[3])
        if STOP <= 4:
            if is_main:
                store_x(y_p[q_idx * T:(q_idx + 1) * T, :], T)
            return
        for g in need_g:
            wv, wt, _, _ = panel(win_k[:, :, 2560 + 256 * g: 2560 + 256 * (g + 1)], (128, 8, 256), key=("v", g))
            d = (1, 4, 16)[g]
            if g < 2:
                slot = tau % 2
                for blk in range(4):
                    b = ps_alloc()
                    for k in range(8):
                        lhs = hT[:, k, blk * 128:(blk + 1) * 128] if g == 0 else hT[:, k, blk::4]
                        mm(ps[:, b, 0:256], lhs, wv[:, k, :], b, [wt, ("hT", k)])
                    hb = slot * 4 + blk
                    P.add("act", lambda e, b=b, g=g, hb=hb: e.activation(out=VH[g][:, hb, :], in_=ps[:, b, 0:256], func=AF.Copy), reads=pst(b), writes=[("V", g, hb)])
                    if is_main and not os.environ.get("KNOVO"):
                        s_ = state["vf"]
                        state["vf"] = 1 - s_
                        P.add("dve", lambda e, b=b, s_=s_: e.tensor_copy(out=vf[:, s_, :], in_=ps[:, b, 0:256]), reads=pst(b), writes=[("vf", s_)])
                        base = q_idx * T
                        if g == 0:
                            dst = ovp[0][base + blk * 128: base + (blk + 1) * 128, :]
                        else:
                            dst = ovp[1][base: base + T, :].rearrange("(i r) c -> r i c", r=4)[blk]
                        dma(os.environ.get("KVQ", "act"), dst, vf[:, s_, :], reads=[("vf", s_)], cls="st")
            else:
                n_ = 1 if is_main else 0
                qq = tau % 4
                for rg in range(8):
                    b = ps_alloc()
                    for r4_ in range(2):
                        r = rg * 2 + r4_
                        for k in range(8):
                            mm(ps[32 * qq: 32 * qq + 32, b, r4_ * 256:(r4_ + 1) * 256], hT[:, k, r::16], wv[:, k, :], b, [wt, ("hT", k)], tp=(0, 32 * qq))
                    o = VH[2][32 * qq: 32 * qq + 32, n_ * 16 + rg * 2: n_ * 16 + rg * 2 + 2, :]
                    i_ = ps[32 * qq: 32 * qq + 32, b, :].rearrange("p (r c) -> p r c", r=2)
                    P.add("act", lambda e, o=o, i_=i_: e.activation(out=o, in_=i_, func=AF.Copy), reads=pst(b),
                          writes=[("V", 2, n_ * 16 + rg * 2 + x_) for x_ in range(2)])
                    if is_main:
                        s_ = state["vf"]
                        state["vf"] = 1 - s_
                        P.add("dve", lambda e, i_=i_, qq=qq, s_=s_: e.tensor_copy(out=sqt[32 * qq: 32 * qq + 32, 2 * s_: 2 * s_ + 2, :], in_=i_),
                              reads=pst(b), writes=[("sqt", s_)])
                        base = q_idx * T
                        dst = ovp[2][base: base + T, :].rearrange("(i r) c -> i r c", r=16)[:, rg * 2: rg * 2 + 2, :]
                        dma("act", dst, sqt[32 * qq: 32 * qq + 32, 2 * s_: 2 * s_ + 2, :], reads=[("sqt", s_)], cls="st")
        if not is_main:
            return
        if STOP <= 5:
            store_x(y_p[q_idx * T:(q_idx + 1) * T, :], T)
            return
        for c in range(4):
            bcv = ps_alloc()
            for j in range(31):
                sl_ = state.get("dg", 0)
                state["dg"] = (sl_ + 1) % 8
                wcol = cols[:, C_CW + c * 31 + j: C_CW + c * 31 + j + 1]
                if j % 2 == 0:
                    P.add("dve", lambda e, sl_=sl_, wcol=wcol: e.tensor_scalar(out=dg[:, sl_, :], in0=ident_b[:], scalar1=wcol, scalar2=None, op0=ALU.mult),
                          reads=["ident_b", "cols"], writes=[("dg", sl_)])
                else:
                    P.add("act", lambda e, sl_=sl_, wcol=wcol: e.activation(out=dg[:, sl_, :], in_=ident_b[:], func=AF.Copy, scale=wcol),
                          reads=["ident_b", "cols"], writes=[("dg", sl_)])
                mm(ps[:, bcv, :], dg[:, sl_, :], uTb[:, c, j: j + T], bcv, [("dg", sl_), "uTb"])
            P.add("act", lambda e, bcv=bcv, c=c: e.activation(out=cacc[:, c, :], in_=ps[:, bcv, :], func=AF.Identity, bias=cols[:, C_CB + c: C_CB + c + 1], scale=1.0),
                  reads=pst(bcv) + ["cols"], writes=[("cacc", c)])
        if tau == n_halo + n_main - 1:
            b = ps_alloc()
            for c in range(4):
                tr(ps[0:30, b, c * 128:(c + 1) * 128], utail[:, c, :], ident_f[:], b, ["utail", "ident_f"])
            P.add("act", lambda e, b=b: e.activation(out=vf[0:30, :, :].rearrange("p a c -> p (a c)"), in_=ps[0:30, b, :], func=AF.Copy), reads=pst(b),
                  writes=[("vf", 0), ("vf", 1)])
            dma("act", oconv_p[:, :], vf[0:30, :, :].rearrange("p a c -> p (a c)"), reads=[("vf", 0), ("vf", 1)], cls="st")
        layer_norm_silu(T)
        if STOP <= 6:
            store_x(y_p[q_idx * T:(q_idx + 1) * T, :], T)
            return
        attention(tau)
        if STOP <= 7:
            store_x(y_p[q_idx * T:(q_idx + 1) * T, :], T)
            return
        merge_and_out(T)
        ffn(T, C_F2N, f2wi, f2wo)
        store_x(y_p[q_idx * T:(q_idx + 1) * T, :], T)

    csq = xin[:, :, :].rearrange("p a (b n) -> p (a b) n", b=2)

    def layer_norm_silu(N):
        xtok = [("xin", 0), ("xin", 1)]
        P.add("act", lambda e: e.activation(out=csq[:, :, 0:N], in_=cacc[:, :, 0:N], func=AF.Square), reads=[("cacc", c) for c in range(4)], writes=xtok)
        bm = ps_alloc()
        for c in range(4):
            mm(ps[:, bm, 0:N], ones_f[:], cacc[:, c, 0:N], bm, ["ones_f", ("cacc", c)])
        bs = ps_alloc()
        for c in range(4):
            mm(ps[:, bs, 0:N], ones_f[:], csq[:, c, 0:N], bs, ["ones_f"] + xtok)
        P.add("act", lambda e: e.activation(out=sg[:, 0, 0:N], in_=ps[:, bm, 0:N], func=AF.Copy, scale=1.0 / 512), reads=pst(bm), writes=[("sg", 0)])
        P.add("dve", lambda e: e.tensor_tensor(out=sg[:, 1, 0:N], in0=sg[:, 0, 0:N], in1=sg[:, 0, 0:N], op=ALU.mult), reads=[("sg", 0)], writes=[("sg", 1)])
        P.add("dve", lambda e: e.scalar_tensor_tensor(out=rstd[:, 0:N], in0=ps[:, bs, 0:N], scalar=1.0 / 512, in1=sg[:, 1, 0:N], op0=ALU.mult, op1=ALU.subtract),
              reads=pst(bs) + [("sg", 1)], writes=["rstd"])
        P.add("act", lambda e: e.activation(out=rstd[:, 0:N], in_=rstd[:, 0:N], func=AF.Sqrt, bias=epsc[:, 0:1], scale=1.0), reads=["rstd", "epsc"], writes=["rstd"])
        P.add("dve", lambda e: e.reciprocal(out=rstd[:, 0:N], in_=rstd[:, 0:N]), reads=["rstd"], writes=["rstd"])
        for c in range(4):
            P.add("dve", lambda e, c=c: e.tensor_tensor(out=cacc[:, c, 0:N], in0=cacc[:, c, 0:N], in1=sg[:, 0, 0:N], op=ALU.subtract),
                  reads=[("cacc", c), ("sg", 0)], writes=[("cacc", c)])
            P.add("dve", lambda e, c=c: e.tensor_tensor(out=cacc[:, c, 0:N], in0=cacc[:, c, 0:N], in1=rstd[:, 0:N], op=ALU.mult),
                  reads=[("cacc", c), "rstd"], writes=[("cacc", c)])
            P.add("act", lambda e, c=c: e.activation(out=sT[:, c, 0:N], in_=cacc[:, c, 0:N], func=AF.Silu, bias=cols[:, C_LNB + c: C_LNB + c + 1],
                                                      scale=cols[:, C_LNG + c: C_LNG + c + 1]),
                  reads=[("cacc", c), "cols"], writes=[("sT", c)])

    def attention(tau):
        slot01 = tau % 2
        prev01 = 1 - slot01
        first_main = (tau == n_halo)
        for h in range(4):
            hp = (h % 2) * 64
            ch = h // 2
            state["psb"] = 0
            bnum = ps_alloc()
            bden = ps_alloc()
            for g in range(3):
                qT = R[hp: hp + 64, 16 + 2 * g + ch, :]
                qtok = ("R", 16 + 2 * g + ch)
                bS = ps_alloc(2)
                S2 = ps[:, bS: bS + 2, :].rearrange("p b n -> p (b n)")
                nblk = 4 if g < 2 else 16
                nq = 128 if g < 2 else 32
                for blk in range(nblk):
                    bank = bS + (blk * 2 * nq) // 512
                    off = (blk * 2 * nq) % 512
                    if g == 0:
                        qcols = qT[:, blk * 128:(blk + 1) * 128]
                        if blk == 0:
                            kprev = KT[0][hp: hp + 64, ch, prev01 * T + 384: prev01 * T + 512]
                            kptok = ("KT", 0, prev01)
                        else:
                            kprev = KT[0][hp: hp + 64, ch, slot01 * T + (blk - 1) * 128: slot01 * T + blk * 128]
                            kptok = ("KT", 0, slot01)
                        kown = KT[0][hp: hp + 64, ch, slot01 * T + blk * 128: slot01 * T + (blk + 1) * 128]
                        kotok = ("KT", 0, slot01)
                    elif g == 1:
                        qcols = qT[:, blk::4]
                        kprev = KT[1][hp: hp + 64, ch, prev01 * T + blk: (prev01 + 1) * T: 4]
                        kptok = ("KT", 1, prev01)
                        kown = KT[1][hp: hp + 64, ch, slot01 * T + blk: (slot01 + 1) * T: 4]
                        kotok = ("KT", 1, slot01)
                    else:
                        qcols = qT[:, blk::16]
                        kprev = KT[2][hp: hp + 64, ch, blk: 4 * T: 16]
                        kptok = ("KT", 2, 0)
                        kown = KT[2][hp: hp + 64, ch, 4 * T + blk: 8 * T: 16]
                        kotok = ("KT", 2, 4)
                    kt_reads = [("KT", 2, s_) for s_ in range(8)] if g == 2 else [kptok, kotok]
                    mm(ps[:, bank, off: off + nq], kprev, qcols, bank, kt_reads + [qtok])
                    mm(ps[:, bank, off + nq: off + 2 * nq], kown, qcols, bank, kt_reads + [qtok])
                P.add("act", lambda e, S2=S2: e.activation(out=PT[:, :], in_=S2, func=AF.Exp, scale=0.125), reads=pst(bS, 2), writes=["PT"])
                if g < 2:
                    for blk in range(4):
                        halo_prev = first_main and ((g == 0 and blk == 0) or g == 1)
                        mo = 256 if halo_prev else 0
                        P.add("dve", lambda e, blk=blk, mo=mo: e.tensor_tensor(out=PT[:, blk * 256:(blk + 1) * 256], in0=PT[:, blk * 256:(blk + 1) * 256],
                                                                             in1=masks[:, mo: mo + 256], op=ALU.mult),
                              reads=["PT", "masks"], writes=["PT"])
                else:
                    qq = tau % 4
                    mview = masks[:, 256:512].rearrange("p (a n) -> p a n", a=2)[:, :, 32 * qq: 32 * qq + 32]
                    P.add("dve", lambda e, mview=mview: e.tensor_tensor(out=PT[:, :].rearrange("p (r a n) -> p r a n", a=2, n=32), in0=PT[:, :].rearrange("p (r a n) -> p r a n", a=2, n=32),
                                                                      in1=mview.unsqueeze(1).to_broadcast([128, 16, 2, 32]), op=ALU.mult),
                          reads=["PT", "masks"], writes=["PT"])
                for blk in range(nblk):
                    off = blk * 2 * nq
                    if g == 0:
                        ocols = slice(blk * 128, (blk + 1) * 128)
                        vprev = VH[0][:, (prev01 * 4 + 3) if blk == 0 else (slot01 * 4 + blk - 1), h * 64:(h + 1) * 64]
                        vown = VH[0][:, slot01 * 4 + blk, h * 64:(h + 1) * 64]
                        vtoks = [("V", 0, (prev01 * 4 + 3) if blk == 0 else (slot01 * 4 + blk - 1)), ("V", 0, slot01 * 4 + blk)]
                    elif g == 1:
                        ocols = slice(blk, T, 4)
                        vprev = VH[1][:, prev01 * 4 + blk, h * 64:(h + 1) * 64]
                        vown = VH[1][:, slot01 * 4 + blk, h * 64:(h + 1) * 64]
                        vtoks = [("V", 1, prev01 * 4 + blk), ("V", 1, slot01 * 4 + blk)]
                    else:
                        ocols = slice(blk, T, 16)
                        vprev = VH[2][:, blk, h * 64:(h + 1) * 64]
                        vown = VH[2][:, 16 + blk, h * 64:(h + 1) * 64]
                        vtoks = [("V", 2, blk), ("V", 2, 16 + blk)]
                    pp = PT[:, off: off + nq]
                    po = PT[:, off + nq: off + 2 * nq]
                    mm(ps[0:64, bnum, ocols], vprev, pp, bnum, vtoks + ["PT"])
                    mm(ps[0:64, bnum, ocols], vown, po, bnum, vtoks + ["PT"])
                    mm(ps[0:64, bden, ocols], ones_b[:, 0:64], pp, bden, ["ones_b", "PT"])
                    mm(ps[0:64, bden, ocols], ones_b[:, 0:64], po, bden, ["ones_b", "PT"])
            P.add("dve", lambda e, bden=bden: e.reciprocal(out=rstd[0:64, :], in_=ps[0:64, bden, :]), reads=pst(bden), writes=["rstd"])
            P.add("dve", lambda e, bnum=bnum, h=h: e.tensor_tensor(out=oT[:, h, :], in0=ps[0:64, bnum, :], in1=rstd[0:64, :], op=ALU.mult),
                  reads=pst(bnum) + ["rstd"], writes=[("oT", h)])

    def merge_and_out(N):
        wco_v = wco.rearrange("(k p) (c n) -> p k c n", p=128, n=128)
        wao_v = wao.rearrange("(h p) (c n) -> p h c n", p=64, n=128)
        for c in range(8):
            wv, wt, _, _ = panel(wco_v[:, :, c, :], (128, 4, 128), key=("wco", c))
            ba = ps_alloc()
            for k in range(4):
                mm(ps[:, ba, 0:N], wv[:, k, :], sT[:, k, 0:N], ba, [wt, ("sT", k)])
            wv2, wt2, _, _ = panel(wao_v[:, :, c, :], (64, 4, 128), key=("wao", c))
            bb = ps_alloc()
            for h in range(4):
                mm(ps[:, bb, 0:N], wv2[:, h, :], oT[:, h, 0:N], bb, [wt2, ("oT", h)])
            P.add("dve", lambda e, ba=ba, c=c: e.tensor_tensor(out=sg[:, 0, 0:N], in0=R[:, c, 0:N], in1=ps[:, ba, 0:N], op=ALU.mult),
                  reads=pst(ba) + [("R", c)], writes=[("sg", 0)])
            P.add("dve", lambda e, bb=bb, c=c: e.tensor_tensor(out=sg[:, 1, 0:N], in0=R[:, 8 + c, 0:N], in1=ps[:, bb, 0:N], op=ALU.mult),
                  reads=pst(bb) + [("R", 8 + c)], writes=[("sg", 1)])
            P.add("dve", lambda e, c=c: e.tensor_tensor(out=hT[:, c, 0:N], in0=sg[:, 0, 0:N], in1=sg[:, 1, 0:N], op=ALU.add),
                  reads=[("sg", 0), ("sg", 1)], writes=[("hT", c)])
        wo_v = wo.rearrange("(k p) (c n) -> p k c n", p=128, n=128)
        for c in range(8):
            wv, wt, _, _ = panel(wo_v[:, :, c, :], (128, 8, 128), key=("wo", c))
            b = ps_alloc()
            for k in range(8):
                mm(ps[:, b, 0:N], wv[:, k, :], hT[:, k, 0:N], b, [wt, ("hT", k)])
            P.add("dve", lambda e, b=b, c=c: e.tensor_tensor(out=xT[:, c, 0:N], in0=ps[:, b, 0:N], in1=xT[:, c, 0:N], op=ALU.add),
                  reads=pst(b) + [("xT", c)], writes=[("xT", c)])

    def sample_tile():
        N = 64
        load_x(xs, N, N)
        ffn(N, C_F1N, f1wi, f1wo)
        rmsnorm(N, C_MIXN, hT, "hT")
        dma("sp", rope[0:64, 0, :], ropes_d[:, :], writes=["rope"])
        P.add("pool", lambda e: e.memset(uT[:], 0.0), writes=["uT"])
        P.add("pool", lambda e: e.memset(KTc[:, :, :, 0:1], 0.0), writes=["uTb", ("KTc", 0, 0), ("KTc", 0, 1), ("KTc", 1, 0), ("KTc", 1, 1)])
        ucat = uT[:, :, 0:544].rearrange("p c (b j) -> p c b j", j=34)
        sc_rows = sconv.rearrange("b j c -> (b j) c")
        for rb in range(4):
            sl = rb % 2
            dma("sp", xin[0:120, sl, 0:512], sc_rows[rb * 120:(rb + 1) * 120, :], writes=[("xin", sl)])
            b = ps_alloc()
            for c in range(4):
                tr(ps[:, b, c * 120:(c + 1) * 120], xin[0:120, sl, c * 128:(c + 1) * 128], ident_f[0:120, 0:120], b, [("xin", sl), "ident_f"])
            P.add("act", lambda e, b=b, rb=rb: e.activation(out=ucat[:, :, 4 * rb: 4 * rb + 4, 0:30],
                                                           in_=ps[:, b, 0:480].rearrange("p (c s j) -> p c s j", c=4, j=30), func=AF.Copy),
                  reads=pst(b), writes=["uT"])
        for c in range(4):
            src = win_k[:, :, 0:1024].rearrange("p k (two j c) -> p k two j c", two=2, c=128)[:, :, :, c, :]
            wv, wt, _, _ = panel(src, (128, 8, 2, 128), key=("u", c))
            ba = ps_alloc()
            for k in range(8):
                mm(ps[:, ba, 0:N], wv[:, k, 0, :], hT[:, k, 0:N], ba, [wt, ("hT", k)])
            bb = ps_alloc()
            for k in range(8):
                mm(ps[:, bb, 0:N], wv[:, k, 1, :], hT[:, k, 0:N], bb, [wt, ("hT", k)])
            P.add("act", lambda e, bb=bb: e.activation(out=sg[:, 0, 0:N], in_=ps[:, bb, 0:N], func=AF.Sigmoid), reads=pst(bb), writes=[("sg", 0)])
            P.add("dve", lambda e, ba=ba, c=c: e.tensor_tensor(out=ucat[:, c, :, 30:34], in0=sg[:, 0, 0:N].rearrange("p (s b) -> p b s", b=16),
                                                              in1=ps[:, ba, 0:N].rearrange("p (s b) -> p b s", b=16), op=ALU.mult),
                  reads=pst(ba) + [("sg", 0)], writes=["uT"])
        bt = ps_alloc()
        for c in range(4):
            P.add("dve", lambda e, c=c: e.tensor_copy(out=sg[:, 1, c * 64:(c + 1) * 64].rearrange("p (s b) -> p b s", b=16), in_=ucat[:, c, :, 30:34]),
                  reads=["uT"], writes=[("sg", 1)])
        for c in range(4):
            tr(ps[0:64, bt, c * 128:(c + 1) * 128], sg[:, 1, c * 64:(c + 1) * 64], ident_f[:], bt, [("sg", 1), "ident_f"])
        P.add("act", lambda e: e.activation(out=xin[0:64, 0, 0:512], in_=ps[0:64, bt, :], func=AF.Copy), reads=pst(bt), writes=[("xin", 0)])
        for s_i in range(4):
            dma("act", oconv_s[:, 26 + s_i, :], xin[16 * s_i: 16 * s_i + 16, 0, 0:512], reads=[("xin", 0)], cls="st")
        for c in range(4):
            oc = cacc[:, c, 0:N].rearrange("p (s b) -> p b s", b=16)
            P.add("dve", lambda e, c=c, oc=oc: e.tensor_scalar(out=oc, in0=ucat[:, c, :, 0:4], scalar1=cols[:, C_CW + c * 31: C_CW + c * 31 + 1],
                                                             scalar2=cols[:, C_CB + c: C_CB + c + 1], op0=ALU.mult, op1=ALU.add),
                  reads=["uT", "cols"], writes=[("cacc", c)])
            for j in range(1, 31):
                P.add("dve", lambda e, c=c, j=j, oc=oc: e.scalar_tensor_tensor(out=oc, in0=ucat[:, c, :, j: j + 4], scalar=cols[:, C_CW + c * 31 + j: C_CW + c * 31 + j + 1],
                                                                          in1=oc, op0=ALU.mult, op1=ALU.add),
                      reads=["uT", "cols", ("cacc", c)], writes=[("cacc", c)])
        for m in range(8):
            src = win_k[:, :, 3328 + 256 * m: 3328 + 256 * (m + 1)].rearrange("p k (two c) -> p k two c", two=2)
            wv, wt, _, _ = panel(src, (128, 8, 2, 128), key=("g", m))
            for t2 in range(2):
                c = 2 * m + t2
                b = ps_alloc()
                for k in range(8):
                    mm(ps[:, b, 0:N], wv[:, k, t2, :], hT[:, k, 0:N], b, [wt, ("hT", k)])
                P.add("act", lambda e, b=b, c=c: e.activation(out=R[:, c, 0:N], in_=ps[:, b, 0:N], func=AF.Sigmoid, bias=cols[:, C_BG + c: C_BG + c + 1], scale=1.0),
                      reads=pst(b) + ["cols"], writes=[("R", c)])
        for g, w_ in enumerate((128, 512, 2048)):
            def kdst(ch, g=g):
                return KTs[:, 2 * g + ch, :], [("KTs", 2 * g + ch)]

            def kout(qn_, g=g, w_=w_):
                for s_i in range(4):
                    dma("act", oks[g][:, w_ - 4 + s_i, :], qn_[16 * s_i: 16 * s_i + 16, 0, :], reads=["qn"], cls="st")

            qk_panel(N, 1, 1792 + 256 * g, 768 + 256 * g, rope, kdst, None, kout)

            def qdst(ch, g=g):
                return R[:, 16 + 2 * g + ch, 0:N], [("R", 16 + 2 * g + ch)]
            qk_panel(N, 1, 1024 + 256 * g, 256 * g, rope, qdst, None, None)
        for g, w_ in enumerate((128, 512, 2048)):
            wv, wt, _, _ = panel(win_k[:, :, 2560 + 256 * g: 2560 + 256 * (g + 1)], (128, 8, 256), key=("v", g))
            b = ps_alloc()
            for k in range(8):
                mm(ps[0:N, b, 0:256], hT[:, k, 0:N], wv[:, k, :], b, [wt, ("hT", k)])
            P.add("act", lambda e, b=b, g=g: e.activation(out=VS[:, g, :], in_=ps[0:N, b, 0:256], func=AF.Copy), reads=pst(b), writes=[("VS", g)])
            s_ = state["vf"]
            state["vf"] = 1 - s_
            P.add("dve", lambda e, b=b, s_=s_: e.tensor_copy(out=vf[0:N, s_, :], in_=ps[0:N, b, 0:256]), reads=pst(b), writes=[("vf", s_)])
            for s_i in range(4):
                dma("act", ovs[g][:, w_ - 4 + s_i, :], vf[16 * s_i: 16 * s_i + 16, s_, :], reads=[("vf", s_)], cls="st")
        layer_norm_silu(N)
        sample_attention()
        merge_and_out(N)
        ffn(N, C_F2N, f2wi, f2wo)
        store_x(y_s, N)

    def sample_attention():
        N = 64
        state["ps_lo"] = 2
        state["psb"] = 2
        bnum, bden = 0, 1
        state["fresh"][0] = True
        state["fresh"][1] = True
        nmask = smask[0:64, 4:132]
        for h in range(4):
            hp = (h % 2) * 64
            ch = h // 2
            bS = ps_alloc()
            for g in range(3):
                mm(ps[0:64, bS, g * 64:(g + 1) * 64], KTs[hp: hp + 64, 2 * g + ch, :], R[hp: hp + 64, 16 + 2 * g + ch, 0:N], bS,
                   [("KTs", 2 * g + ch), ("R", 16 + 2 * g + ch)])
            P.add("act", lambda e, bS=bS: e.activation(out=PTs[0:64, :], in_=ps[0:64, bS, 0:192], func=AF.Exp, scale=0.125), reads=pst(bS), writes=["PTs"])
            P.add("dve", lambda e: e.tensor_tensor(out=PTs[0:64, 0:64], in0=PTs[0:64, 0:64], in1=nmask[:, 0:64], op=ALU.mult), reads=["PTs", "smask"], writes=["PTs"])
            P.add("dve", lambda e: e.tensor_tensor(out=PTs[0:64, 64:192].rearrange("p (g n) -> p g n", g=2), in0=PTs[0:64, 64:192].rearrange("p (g n) -> p g n", g=2),
                                                   in1=nmask[:, 64:128].unsqueeze(1).to_broadcast([64, 2, 64]), op=ALU.mult), reads=["PTs", "smask"], writes=["PTs"])
            for g in range(3):
                mm(ps[0:64, bnum, h * 64:(h + 1) * 64], VS[:, g, h * 64:(h + 1) * 64], PTs[0:64, g * 64:(g + 1) * 64], bnum, [("VS", g), "PTs"])
                mm(ps[0:64, bden, h * 64:(h + 1) * 64], ones_b[0:64, 0:64], PTs[0:64, g * 64:(g + 1) * 64], bden, ["ones_b", "PTs"])
        for b in range(16):
            pans = {}
            for nm, srcs in (("K", ck), ("V", cv)):
                sA = state["stg"]; state["stg"] = (sA + 1) % NSTG
                wA = take_wp()
                dma("sp", stg[:, sA, 0:256], srcs[0][b, :, :], writes=[("stg", sA)])
                dma("sp", stg[:, sA, 256:1280], srcs[1][b, :, :].rearrange("(i s) c -> i (s c)", s=4), reads=[("stg", sA)], writes=[("stg", sA)])
                cast_op("dve" if nm == "K" else "act", wp[:, wA, 0:1280], stg[:, sA, 0:1280], [("stg", sA)], [("wp", wA)])
                sB = state["stg"]; state["stg"] = (sB + 1) % NSTG
                wB = take_wp()
                dma("sp", stg[:, sB, 0:1024].rearrange("p (s c) -> p s c", s=4), srcs[2][b, :, :].rearrange("(i r) c -> i r c", r=16)[:, 0:4, :], writes=[("stg", sB)])
                cast_op("act" if nm == "K" else "dve", wp[:, wB, 0:1024], stg[:, sB, 0:1024], [("stg", sB)], [("wp", wB)])
                pans[nm] = (wA, wB)

            def blk_ap(nm, blk, c0, c1):
                wA, wB = pans[nm]
                if blk < 5:
                    return wp[:, wA, blk * 256 + c0: blk * 256 + c1], ("wp", wA)
                return wp[:, wB, (blk - 5) * 256 + c0: (blk - 5) * 256 + c1], ("wp", wB)
            for ch in range(2):
                for part in range(2):
                    blks = range(0, 5) if part == 0 else range(5, 9)
                    bt = ps_alloc()
                    pb = ps[:, bt, :].bitcast(BF16)
                    for i_b, blk in enumerate(blks):
                        ap_, tk = blk_ap("K", blk, ch * 128, (ch + 1) * 128)
                        tr(pb[:, i_b * 128:(i_b + 1) * 128], ap_, ident_b[:], bt, [tk, "ident_b"])
                    nbk = len(blks)
                    o = KTc[:, ch, blks[0]: blks[0] + nbk, :]
                    i_ = pb[:, 0: nbk * 128].rearrange("p (k n) -> p k n", n=128)
                    if part == 0:
                        P.add("act", lambda e, o=o, i_=i_: e.activation(out=o, in_=i_, func=AF.Copy), reads=pst(bt), writes=[("KTc", ch, part)])
                    else:
                        P.add("dve", lambda e, o=o, i_=i_: e.tensor_copy(out=o, in_=i_), reads=pst(bt), writes=[("KTc", ch, part)])
            bS = ps_alloc()
            for h in range(4):
                hp = (h % 2) * 64
                ch = h // 2
                kt_r = [("KTc", ch, 0), ("KTc", ch, 1)]
                c0 = h * 12
                q0 = R[hp: hp + 64, 16 + ch, b:N:16]
                mm(ps[:, bS, c0: c0 + 4], KTc[hp: hp + 64, ch, 0, :], q0, bS, kt_r + [("R", 16 + ch)])
                for g in (1, 2):
                    for s_i in range(4):
                        qc = R[hp: hp + 64, 16 + 2 * g + ch, 16 * s_i + b: 16 * s_i + b + 1]
                        mm(ps[:, bS, c0 + 4 * g + s_i: c0 + 4 * g + s_i + 1], KTc[hp: hp + 64, ch, 1 + 4 * (g - 1) + s_i, :], qc, bS, kt_r + [("R", 16 + 2 * g + ch)])
            P.add("act", lambda e, bS=bS: e.activation(out=PTs[:, 0:48], in_=ps[:, bS, 0:48], func=AF.Exp, scale=0.125), reads=pst(bS), writes=["PTs"])
            P.add("dve", lambda e: e.tensor_tensor(out=PTs[:, 0:48].rearrange("p (h n) -> p h n", h=4)[:, :, 0:4], in0=PTs[:, 0:48].rearrange("p (h n) -> p h n", h=4)[:, :, 0:4],
                                                   in1=smask[:, 0:4].unsqueeze(1).to_broadcast([128, 4, 4]), op=ALU.mult), reads=["PTs", "smask"], writes=["PTs"])
            for h in range(4):
                c0 = h * 12
                v0, tk0 = blk_ap("V", 0, h * 64, (h + 1) * 64)
                mm(ps[0:64, bnum, h * 64 + b: (h + 1) * 64: 16], v0, PTs[:, c0: c0 + 4], bnum, [tk0, "PTs"])
                mm(ps[0:64, bden, h * 64 + b: (h + 1) * 64: 16], ones_b[:, 0:64], PTs[:, c0: c0 + 4], bden, ["ones_b", "PTs"])
                for g in (1, 2):
                    for s_i in range(4):
                        vb, tkb = blk_ap("V", 1 + 4 * (g - 1) + s_i, h * 64, (h + 1) * 64)
                        col = h * 64 + 16 * s_i + b
                        pc_ = PTs[:, c0 + 4 * g + s_i: c0 + 4 * g + s_i + 1]
                        mm(ps[0:64, bnum, col: col + 1], vb, pc_, bnum, [tkb, "PTs"])
                        mm(ps[0:64, bden, col: col + 1], ones_b[:, 0:64], pc_, bden, ["ones_b", "PTs"])
        P.add("dve", lambda e: e.reciprocal(out=rstd[0:64, 0:256], in_=ps[0:64, bden, 0:256]), reads=pst(bden), writes=["rstd"])
        P.add("dve", lambda e: e.tensor_tensor(out=oT[:, :, 0:N], in0=ps[0:64, bnum, 0:256].rearrange("p (h n) -> p h n", h=4),
                                               in1=rstd[0:64, 0:256].rearrange("p (h n) -> p h n", h=4), op=ALU.mult),
              reads=pst(bnum) + ["rstd"], writes=[("oT", h_) for h_ in range(4)])
        state["ps_lo"] = 0

    for tau in range(n_halo + n_main):
        prompt_tile(tau)
    drip(1000)
    if do_sample:
        sample_tile()
    flush_pending(0)

    P.emit()
    st.close()
    return P


_CACHE = {}


def _consts():
    j = np.arange(128)[:, None]
    i = np.arange(128)[None, :]
    mprev = (j >= i).astype(np.float32)
    mown = (j <= i).astype(np.float32)
    return mprev, mown


def kernel(_cores=None, **inp):
    f32 = lambda a: np.ascontiguousarray(np.asarray(a, dtype=np.float32))
    cores = list(range(NCORES)) if _cores is None else list(_cores)
    x_prompt = f32(inp["x_prompt"])
    x_sample = f32(inp["x_sample"])
    if "nc" not in _CACHE:
        nc = bass.Bass("TRN2", target_bir_lowering=False)
        import os
        cfg = os.environ.get("KCFG")
        if cfg:
            a, b_, c_ = [int(v) for v in cfg.split(",")]
            P = build(nc, a, b_, bool(c_))
        else:
            P = build(nc)
        print("prog stats", P.stats)
        _CACHE["nc"] = nc
    nc = _CACHE["nc"]
    mprev, mown = _consts()
    colv = np.zeros((128, NCOLS), np.float32)
    pc = lambda v, n: f32(v).reshape(n, 128).T
    colv[:, C_F1N:C_F1N + 8] = pc(inp["ffn1_norm"][0], 8)
    colv[:, C_MIXN:C_MIXN + 8] = pc(inp["mix_norm"][0], 8)
    colv[:, C_F2N:C_F2N + 8] = pc(inp["ffn2_norm"][0], 8)
    colv[:, C_BG:C_BG + 16] = pc(inp["b_gate"][0], 16)
    colv[:, C_CB:C_CB + 4] = pc(inp["conv_b"][0], 4)
    colv[:, C_LNG:C_LNG + 4] = pc(inp["conv_ln_g"][0], 4)
    colv[:, C_LNB:C_LNB + 4] = pc(inp["conv_ln_b"][0], 4)
    cw = f32(inp["conv_w"][0])
    colv[:, C_CW:C_CW + 124] = cw.reshape(31, 4, 128).transpose(2, 1, 0).reshape(128, 124)
    qkg = np.broadcast_to(np.concatenate([f32(inp["q_norm"][0]).reshape(768), f32(inp["k_norm"][0]).reshape(768)])[None, :], (128, 1536)).copy()
    inv = (np.float32(500000.0) ** (-np.arange(8, dtype=np.float32) * np.float32(2.0 / 16))).astype(np.float32)

    def rope_tab(pos):
        ang = (pos.astype(np.float32)[:, None] * inv[None, :]).astype(np.float32)
        return np.concatenate([np.cos(ang), np.sin(ang)], axis=1).astype(np.float32)

    ident = np.eye(128, dtype=np.float32)
    smask = np.zeros((128, 132), np.float32)
    smask[:, 0:4] = (np.arange(128)[:, None] >= np.arange(4)[None, :]).astype(np.float32)
    n_ = np.arange(64)
    sb_, bb_ = n_ // 16, n_ % 16
    smask[0:64, 4:68] = ((bb_[:, None] == bb_[None, :]) & (sb_[:, None] <= sb_[None, :])).astype(np.float32)
    smask[0:64, 68:132] = np.eye(64, dtype=np.float32)
    shared = dict(
        f1wi=f32(inp["ffn1_w_in"][0]), f1wo=f32(inp["ffn1_w_out"][0]), win=f32(inp["w_in"][0]), wco=f32(inp["w_conv_out"][0]),
        wao=f32(inp["w_attn_out"][0]), wo=f32(inp["w_out"][0]), f2wi=f32(inp["ffn2_w_in"][0]), f2wo=f32(inp["ffn2_w_out"][0]),
        cols=colv, qkg=qkg, ident=ident, smask=smask,
    )
    in_maps = []
    for c in cores:
        b, half = c // 2, c % 2
        xpc = np.zeros((4096, D), np.float32)
        if half == 0:
            xpc[2048:] = x_prompt[b, 0:2048]
        else:
            xpc[:] = x_prompt[b]
        import os
        if os.environ.get("KCFG"):
            a, b_, c_ = [int(v) for v in os.environ["KCFG"].split(",")]
            xpc = xpc[: (a + b_) * T]
        pos = np.arange(4096) + (half * 2048 - 2048)
        m = np.zeros((128, 512), np.float32)
        m[:, 0:128] = mprev
        m[:, 128:256] = mown
        m[:, 256:384] = mprev * float(half)
        m[:, 384:512] = mown
        d = dict(shared)
        d.update(xp=xpc, xs=f32(x_sample[16 * c:16 * c + 16].transpose(1, 0, 2)).reshape(64, D), ropep=rope_tab(np.maximum(pos, 0))[: xpc.shape[0]],
                 ropes=rope_tab(2048 + np.repeat(np.arange(4), 16)), masks=m,
                 sconv=f32(inp["state_conv"][0, 16 * c:16 * c + 16]))
        for w in (128, 512, 2048):
            d[f"ck{w}"] = f32(inp[f"cache_k_w{w}"][0, 16 * c:16 * c + 16]).reshape(16, w, 256)
            d[f"cv{w}"] = f32(inp[f"cache_v_w{w}"][0, 16 * c:16 * c + 16]).reshape(16, w, 256)
        in_maps.append(d)
    res = run_bass_kernel_spmd(nc, in_maps, core_ids=list(range(len(cores))))
    rs = dict(zip(cores, res.results))
    y_p = np.zeros((4, 4096, D), np.float32)
    y_s = np.zeros((128, 4, D), np.float32)
    outs_kv_p = [np.zeros((1, 4, w, 4, 64), np.float32) for w in (128, 128, 512, 512, 2048, 2048)]
    conv_p = np.zeros((1, 4, 30, 512), np.float32)
    outs_kv_s = [np.zeros((1, 128, w, 4, 64), np.float32) for w in (128, 128, 512, 512, 2048, 2048)]
    conv_s = np.zeros((1, 128, 30, 512), np.float32)
    for c in cores:
        b, half = c // 2, c % 2
        r = rs[c]
        y_p[b, half * 2048:(half + 1) * 2048] = r["y_p"]
        y_s[16 * c:16 * c + 16] = r["y_s"].reshape(4, 16, D).transpose(1, 0, 2)
        if half == 1:
            for g, w in enumerate((128, 512, 2048)):
                outs_kv_p[2 * g][0, b] = r[f"okp{g}"][2048 - w:].reshape(w, 4, 64)
                outs_kv_p[2 * g + 1][0, b] = r[f"ovp{g}"][2048 - w:].reshape(w, 4, 64)
            conv_p[0, b] = r["oconv_p"]
        for g, w in enumerate((128, 512, 2048)):
            outs_kv_s[2 * g][0, 16 * c:16 * c + 16] = r[f"oks{w}"].reshape(16, w, 4, 64)
            outs_kv_s[2 * g + 1][0, 16 * c:16 * c + 16] = r[f"ovs{w}"].reshape(16, w, 4, 64)
        conv_s[0, 16 * c:16 * c + 16] = r["oconv_s"]
    return (y_p, y_s, *outs_kv_p, conv_p, *outs_kv_s, conv_s)
```
